# Optimizing a Trainium2 kernel written in Bass

```python
import jax, jax.numpy as jnp
from jax import lax
import numpy as np

D_MODEL = 1024
BATCH = 8
SEQ = 4096
DEPTH = 4

N_HEADS = 8
HEAD_DIM = 64
V_DIM = 2 * HEAD_DIM
D_ATTN = N_HEADS * V_DIM
D_QK = N_HEADS * 2 * HEAD_DIM
D_CONV = D_MODEL
CONV_WIDTH = 3
D_FF = 2816
ROPE_THETA = 10000.0
Q_BLOCK = 128
EPS = 1e-6
N_MOD = 9
D_IN = 3 * D_CONV + 2 * D_QK + D_ATTN + 2 * D_MODEL

kernel_name = "hybrid_shortconv_diffattn_macaron_adaln"


def rms_norm(x, g):
    xf = x.astype(jnp.float32)
    xf = xf * lax.rsqrt(jnp.mean(xf * xf, axis=-1, keepdims=True) + EPS)
    return xf.astype(x.dtype) * g


def rope_tables(positions, dtype):
    inv_freq = 1.0 / (ROPE_THETA ** (jnp.arange(0, HEAD_DIM // 2, dtype=jnp.float32) * (2.0 / HEAD_DIM)))
    ang = positions.astype(jnp.float32)[..., None] * inv_freq
    return (jnp.cos(ang)[:, :, None, None, :].astype(dtype),
            jnp.sin(ang)[:, :, None, None, :].astype(dtype))


def apply_rope(t, cos, sin):
    t1, t2 = jnp.split(t, 2, axis=-1)
    return jnp.concatenate([t1 * cos - t2 * sin, t2 * cos + t1 * sin], axis=-1)


def swiglu(h, w_gu, w_down):
    g, u = jnp.split(h @ w_gu, 2, axis=-1)
    return (jax.nn.silu(g) * u) @ w_down


def causal_depthwise_conv(u, w):
    return lax.conv_general_dilated(
        u, w[:, None, :].astype(u.dtype), window_strides=(1,),
        padding=((CONV_WIDTH - 1, 0),), dimension_numbers=("NWC", "WIO", "NWC"),
        feature_group_count=u.shape[-1])


def diff_attention(q, k, v, lam):
    qh = q.transpose(0, 2, 3, 1, 4)
    kh = k.transpose(0, 2, 3, 1, 4)
    vh = v.transpose(0, 2, 1, 3)
    seq = q.shape[1]
    outs = []
    for start in range(0, seq, Q_BLOCK):
        end = start + Q_BLOCK
        s = jnp.einsum('bhcqd,bhckd->bhcqk', qh[:, :, :, start:end], kh[:, :, :, :end]).astype(jnp.float32)
        causal = jnp.arange(end)[None, :] <= jnp.arange(start, end)[:, None]
        p = jax.nn.softmax(jnp.where(causal, s, -jnp.inf), axis=-1)
        a = p[:, :, 0] - lam * p[:, :, 1]
        outs.append(jnp.einsum('bhqk,bhkd->bhqd', a.astype(vh.dtype), vh[:, :, :end]))
    return jnp.concatenate(outs, axis=2)


def split_combined(proj):
    sizes = (D_CONV, D_CONV, D_CONV, D_QK, D_QK, D_ATTN, D_MODEL, D_MODEL)
    idx = [int(i) for i in np.cumsum(sizes)[:-1]]
    return jnp.split(proj, idx, axis=-1)


def setup_inputs(seed: int = 0) -> dict:
    key = jax.random.key(seed)
    ks = jax.random.split(key, 24)
    f32 = jnp.float32

    def nrm(k, shape, fan_in, gain=1.0):
        return jax.random.normal(k, shape, f32) * (gain * fan_in ** -0.5)

    x = jax.random.normal(ks[0], (BATCH, SEQ, D_MODEL), f32)
    c = jax.random.normal(ks[1], (BATCH, D_MODEL), f32)
    offs = jax.random.randint(ks[2], (BATCH, 1), 0, 1024, dtype=jnp.int32)
    positions = offs + jnp.arange(SEQ, dtype=jnp.int32)[None, :]
    return {
        "x": x,
        "c": c,
        "positions": positions,
        "norm_g": 1.0 + 0.02 * jax.random.normal(ks[3], (DEPTH, 3, D_MODEL), f32),
        "w_ada": nrm(ks[4], (DEPTH, D_MODEL, N_MOD * D_MODEL), D_MODEL, 0.5),
        "b_ada": 0.02 * jax.random.normal(ks[5], (DEPTH, N_MOD * D_MODEL), f32),
        "w_ffn1_gu": nrm(ks[6], (DEPTH, D_MODEL, 2 * D_FF), D_MODEL),
        "w_ffn1_down": nrm(ks[7], (DEPTH, D_FF, D_MODEL), D_FF),
        "w_in": nrm(ks[8], (DEPTH, D_MODEL, D_IN), D_MODEL),
        "conv_w": nrm(ks[9], (DEPTH, CONV_WIDTH, D_CONV), CONV_WIDTH),
        "q_norm_g": 1.0 + 0.02 * jax.random.normal(ks[10], (DEPTH, HEAD_DIM), f32),
        "k_norm_g": 1.0 + 0.02 * jax.random.normal(ks[11], (DEPTH, HEAD_DIM), f32),
        "lambda_q1": 0.1 * jax.random.normal(ks[12], (DEPTH, HEAD_DIM), f32),
        "lambda_k1": 0.1 * jax.random.normal(ks[13], (DEPTH, HEAD_DIM), f32),
        "lambda_q2": 0.1 * jax.random.normal(ks[14], (DEPTH, HEAD_DIM), f32),
        "lambda_k2": 0.1 * jax.random.normal(ks[15], (DEPTH, HEAD_DIM), f32),
        "subln_g": 1.0 + 0.02 * jax.random.normal(ks[16], (DEPTH, V_DIM), f32),
        "w_conv_out": nrm(ks[17], (DEPTH, D_CONV, D_MODEL), D_CONV),
        "w_attn_out": nrm(ks[18], (DEPTH, D_ATTN, D_MODEL), D_ATTN),
        "w_o": nrm(ks[19], (DEPTH, D_MODEL, D_MODEL), D_MODEL),
        "w_ffn2_gu": nrm(ks[20], (DEPTH, D_MODEL, 2 * D_FF), D_MODEL),
        "w_ffn2_down": nrm(ks[21], (DEPTH, D_FF, D_MODEL), D_FF),
    }


def reference(x, c, positions, norm_g, w_ada, b_ada, w_ffn1_gu, w_ffn1_down, w_in, conv_w,
              q_norm_g, k_norm_g, lambda_q1, lambda_k1, lambda_q2, lambda_k2, subln_g,
              w_conv_out, w_attn_out, w_o, w_ffn2_gu, w_ffn2_down):
    bsz, seq, _ = x.shape
    cos, sin = rope_tables(positions, x.dtype)
    c_act = jax.nn.silu(c)
    q_scale = HEAD_DIM ** -0.5
    for l in range(DEPTH):
        lam_init = 0.8 - 0.6 * float(np.exp(-0.3 * l))
        mod = (c_act @ w_ada[l] + b_ada[l])[:, None, :]
        (sh1, sc1, g1, sh2, sc2, g2, sh3, sc3, g3) = jnp.split(mod, N_MOD, axis=-1)

        h = rms_norm(x, norm_g[l, 0]) * (1.0 + sc1) + sh1
        x = x + 0.5 * g1 * swiglu(h, w_ffn1_gu[l], w_ffn1_down[l])

        h = rms_norm(x, norm_g[l, 1]) * (1.0 + sc2) + sh2
        b_c, c_c, x_c, q, k, v, gate_conv, gate_attn = split_combined(h @ w_in[l])

        y_conv = b_c * causal_depthwise_conv(c_c * x_c, conv_w[l])
        y_conv = y_conv @ w_conv_out[l]

        q = q.reshape(bsz, seq, N_HEADS, 2, HEAD_DIM)
        k = k.reshape(bsz, seq, N_HEADS, 2, HEAD_DIM)
        v = v.reshape(bsz, seq, N_HEADS, V_DIM)
        q = apply_rope(rms_norm(q, q_norm_g[l]), cos, sin) * q_scale
        k = apply_rope(rms_norm(k, k_norm_g[l]), cos, sin)
        lam = (jnp.exp(jnp.sum(lambda_q1[l].astype(jnp.float32) * lambda_k1[l].astype(jnp.float32)))
               - jnp.exp(jnp.sum(lambda_q2[l].astype(jnp.float32) * lambda_k2[l].astype(jnp.float32)))
               + lam_init)
        o = diff_attention(q, k, v, lam)
        o = rms_norm(o, subln_g[l]) * (1.0 - lam_init)
        o = o.transpose(0, 2, 1, 3).reshape(bsz, seq, D_ATTN)
        y_attn = o @ w_attn_out[l]

        merged = jax.nn.sigmoid(gate_conv) * y_conv + jax.nn.sigmoid(gate_attn) * y_attn
        x = x + g2 * (merged @ w_o[l])

        h = rms_norm(x, norm_g[l, 2]) * (1.0 + sc3) + sh3
        x = x + 0.5 * g3 * swiglu(h, w_ffn2_gu[l], w_ffn2_down[l])
    return x
```

```python
import math
from contextlib import ExitStack

import numpy as np
import concourse.bass as bass
import concourse.mybir as mybir
from concourse.alu_op_type import AluOpType as ALU
from concourse.bass_utils import run_bass_kernel_spmd

F32 = mybir.dt.float32
BF16 = mybir.dt.bfloat16
I32 = mybir.dt.int32
AF = mybir.ActivationFunctionType

P = 128
D = 1024
NCH = 8
DFF = 2816
NF = 22
T = 512
EPS = 1e-6
HEAD_DIM = 64
N_HEADS = 8
WX = 4096
NWL = 60
TWO_PI = 2.0 * math.pi

R_BADA, R_NG, R_CW, R_QG, R_KG, R_SUB = 0, 72, 96, 120, 121, 122


class Res:
    __slots__ = ("name", "w", "r", "frozen")

    def __init__(self, name):
        self.name = name
        self.w = None
        self.r = []
        self.frozen = False


class Eng:
    def __init__(self, name, e, kind):
        self.name, self.e, self.kind = name, e, kind
        self.semidx = None
        self.count = 0
        self.waited = {}
        self.pending = []
        self.dsems = []
        self.duse = []
        self.dnext = 0


class Kern:
    def __init__(self, nc, es, n_dma_sems=12):
        self.nc = nc
        self.sems = []
        self.es = es

        def newsem(name):
            s = es.enter_context(nc.semaphore(name))
            self.sems.append(s)
            return len(self.sems) - 1

        self.PE = Eng("pe", nc.tensor, "pe")
        self.ACT = Eng("act", nc.scalar, "cmp")
        self.DVE = Eng("dve", nc.vector, "cmp")
        self.POOL = Eng("pool", nc.gpsimd, "cmp")
        self.SP = Eng("sp", nc.sync, "dmaq")
        self.compute = [self.PE, self.ACT, self.DVE, self.POOL]
        for E in self.compute:
            E.semidx = newsem("c_" + E.name)
        self.LQ = self.SP
        self.SQ = self.POOL
        for Q in (self.SP, self.POOL):
            Q.dsems = [newsem(f"d_{Q.name}{i}") for i in range(n_dma_sems)]
            Q.duse = [0] * n_dma_sems
        self.all = [self.PE, self.ACT, self.DVE, self.POOL, self.SP]
        self.n_ins = 0

    def _wait(self, E, s, v):
        if v <= 0:
            return
        if E.waited.get(s, 0) < v:
            E.e.wait_ge(self.sems[s], v)
            E.waited[s] = v

    def _deps(self, E, reads, writes):
        deps = {}
        for r in reads:
            if r.w is not None:
                s, v = r.w
                if deps.get(s, 0) < v:
                    deps[s] = v
        for w in writes:
            if w.w is not None:
                s, v = w.w
                if deps.get(s, 0) < v:
                    deps[s] = v
            for (s, v) in w.r:
                if deps.get(s, 0) < v:
                    deps[s] = v
        for s, v in deps.items():
            if E.kind == "pe" and s == E.semidx:
                continue
            self._wait(E, s, v)

    @staticmethod
    def _register(tok, reads, writes):
        for r in reads:
            if not r.frozen:
                r.r.append(tok)
        for w in writes:
            w.w = tok
            w.r = []

    def op(self, E, fn, reads=(), writes=(), signal=True):
        self._deps(E, reads, writes)
        ins = fn(E.e)
        self.n_ins += 1
        if signal:
            E.count += 1
            ins.then_inc(self.sems[E.semidx], 1)
            tok = (E.semidx, E.count)
            for (rs, ws) in E.pending:
                self._register(tok, rs, ws)
            E.pending = []
            self._register(tok, reads, writes)
        else:
            E.pending.append((reads, writes))
        return ins

    def dma(self, Q, out, in_, reads=(), writes=()):
        i = Q.dnext % len(Q.dsems)
        Q.dnext += 1
        sidx = Q.dsems[i]
        self._wait(Q, sidx, Q.duse[i] * 16)
        self._deps(Q, reads, writes)
        Q.e.dma_start(out=out, in_=in_).then_inc(self.sems[sidx], 16)
        self.n_ins += 1
        Q.duse[i] += 1
        tok = (sidx, Q.duse[i] * 16)
        self._register(tok, reads, writes)

    def barrier(self):
        assert not self.PE.pending
        toks = [(E.semidx, E.count) for E in self.compute]
        for Q in (self.SP, self.POOL):
            toks += [(s, u * 16) for s, u in zip(Q.dsems, Q.duse)]
        for E in self.all:
            for s, v in toks:
                if s == E.semidx:
                    continue
                self._wait(E, s, v)


W_IN_OFF = dict(b=0, c=1024, x=2048, q=3072, k=4096, v=5120, gc=6144, ga=7168)


def _pack_layer(inp, l, out, base):
    n = [base]

    def add(arr):
        out[n[0], :, : arr.shape[1]] = arr
        n[0] += 1

    def gu_chunks(w):
        w4 = w.reshape(8, 128, 2, DFF)
        for j in range(11):
            blk = w4[:, :, :, j * 256:(j + 1) * 256]
            add(blk.transpose(1, 0, 2, 3).reshape(128, 4096))

    def down_chunks(w):
        w3 = w.reshape(NF, 128, D)
        for dc in range(8):
            blk = w3[:, :, dc * 128:(dc + 1) * 128]
            add(blk.transpose(1, 0, 2).reshape(128, NF * 128))

    def col_chunks(w, cols_list):
        w3 = w.reshape(8, 128, -1)
        for cols in cols_list:
            blk = np.concatenate([w3[:, :, s:s + 128] for s in cols], axis=2)
            add(blk.transpose(1, 0, 2).reshape(128, 4096))

    gu_chunks(inp["w_ffn1_gu"][l])
    down_chunks(inp["w_ffn1_down"][l])
    w_in = inp["w_in"][l]
    o = W_IN_OFF
    cl = []
    for key in ("q", "k", "v", "ga"):
        for j in range(2):
            cl.append([o[key] + j * 512 + i * 128 for i in range(4)])
    for i in range(8):
        cl.append([o["b"] + i * 128, o["c"] + i * 128, o["x"] + i * 128, o["gc"] + i * 128])
    col_chunks(w_in, cl)
    two = [[j * 512 + i * 128 for i in range(4)] for j in range(2)]
    col_chunks(inp["w_conv_out"][l], two)
    col_chunks(inp["w_attn_out"][l], two)
    col_chunks(inp["w_o"][l], two)
    gu_chunks(inp["w_ffn2_gu"][l])
    down_chunks(inp["w_ffn2_down"][l])
    assert n[0] == base + NWL


def _chunk_len(n):
    n = n % NWL
    if 11 <= n < 19 or 52 <= n < 60:
        return NF * 128
    return 4096


def _consts():
    ident = np.eye(128, dtype=np.float32)
    k = np.arange(128)[:, None]
    m = np.arange(128)[None, :]
    ones = np.ones((128, 128), np.float32)
    blk = (k // 64 == m // 64).astype(np.float32)
    rot = np.zeros((128, 128), np.float32)
    for mm in range(128):
        if mm % 64 < 32:
            rot[mm + 32, mm] = -1.0
        else:
            rot[mm - 32, mm] = 1.0
    tri = (k <= m).astype(np.float32)
    cmat = np.concatenate([ones, blk, rot, tri, tri], axis=1)
    inv = 1.0 / (10000.0 ** (np.arange(0, 32, dtype=np.float32) * (2.0 / HEAD_DIM)))
    invf = np.tile(inv.astype(np.float32), 4).reshape(128, 1)
    return ident, cmat, invf


def _host_inputs(inp, depth):
    wpk = np.zeros((depth * NWL, 128, WX), np.float32)
    for l in range(depth):
        _pack_layer(inp, l, wpk, l * NWL)
    wa = np.asarray(inp["w_ada"][:depth], np.float32).reshape(depth, 8, 128, 18, 512)
    wada = np.ascontiguousarray(wa.transpose(0, 3, 2, 1, 4)).reshape(depth * 18, 128, 4096)
    vecs = np.zeros((depth, 128, 128), np.float32)
    for l in range(depth):
        vecs[l, R_BADA:R_BADA + 72] = inp["b_ada"][l].reshape(72, 128)
        vecs[l, R_NG:R_NG + 24] = inp["norm_g"][l].reshape(24, 128)
        vecs[l, R_CW:R_CW + 24] = inp["conv_w"][l].reshape(24, 128)
        vecs[l, R_QG] = np.tile(inp["q_norm_g"][l], 2)
        vecs[l, R_KG] = np.tile(inp["k_norm_g"][l], 2)
        vecs[l, R_SUB] = inp["subln_g"][l]
    lamv = np.stack([np.stack([inp["lambda_q1"][l], inp["lambda_k1"][l],
                               inp["lambda_q2"][l], inp["lambda_k2"][l]]) for l in range(depth)])
    lamv = np.ascontiguousarray(lamv.reshape(1, depth * 4 * 64).astype(np.float32))
    ident, cmat, invf = _consts()
    return dict(wpk=wpk, wada=wada, vecs=vecs, lamv=lamv, ident=ident, cmat=cmat, invf=invf)


def build(S, depth, stages=("f1", "m1", "m2", "m3", "f2")):
    NT = S // T
    NB = S // 128
    nc = bass.Bass("TRN2", target_bir_lowering=False)

    def din(name, shape, dt=F32):
        return nc.dram_tensor(name, shape, dt, kind="ExternalInput")

    def dscr(name, shape, dt):
        return nc.dram_tensor(name, shape, dt, kind="Internal")

    x_in = din("x", [S, D])
    c_in = din("c", [8, 128])
    pos_in = din("pos", [1, S], I32)
    wpk = din("wpk", [depth * NWL, 128, WX])
    wada = din("wada", [depth * 18, 128, 4096])
    vecs = din("vecs", [depth, 128, 128])
    lamv = din("lamv", [1, depth * 256])
    ident_in = din("ident", [128, 128])
    cmat_in = din("cmat", [128, 640])
    invf_in = din("invf", [128, 1])
    y_out = nc.dram_tensor("y", [S, D], F32, kind="ExternalOutput")

    XT = dscr("XT", [D, S], F32)
    QT = dscr("QT", [D, S], BF16)
    KT = dscr("KT", [D, S], BF16)
    VV = dscr("VV", [S, D], BF16)
    GC = dscr("GC", [D, S], BF16)
    SGA = dscr("SGA", [D, S], BF16)
    OT = dscr("OT", [D, S], BF16)
    COS = dscr("COS", [128, S], F32)
    SIN = dscr("SIN", [128, S], F32)
    WBF = dscr("WBF", [depth * NWL, 128, WX], BF16)

    def fm(tensor):
        return tensor.ap().rearrange("(c p) s -> p c s", p=128)

    with ExitStack() as es:
        K = Kern(nc, es)
        PE, ACT, DVE, SPQ, STQ = K.PE, K.ACT, K.DVE, K.LQ, K.SQ

        uid = {"n": 0}

        def sb(stack, name, shape, dt):
            uid["n"] += 1
            return stack.enter_context(nc.sbuf_tensor(f"s{uid['n']}_{name}", shape, dt))

        ident = sb(es, "ident", [128, 128], F32)
        cbf = sb(es, "cbf", [128, 640], BF16)
        invf = sb(es, "invf", [128, 1], F32)
        VECT = sb(es, "VECT", [128, depth, 128], F32)
        MODB = sb(es, "MODB", [128, depth, 72], F32)
        AV = sb(es, "AV", [128, depth, 24], F32)
        GV = sb(es, "GV", [128, depth, 24], F32)
        QG = sb(es, "QG", [128, depth], F32)
        SUBG = sb(es, "SUBG", [128, depth], F32)
        NEGLAM = sb(es, "NEGLAM", [128, depth], F32)
        NR = 4
        ring = [sb(es, f"ring{i}", [128, WX], BF16) for i in range(NR)]
        psall = es.enter_context(nc.psum_tensor("psall", [128, 8 * 512], F32))
        banks = [psall[:, i * 512:(i + 1) * 512] for i in range(8)]

        r_const = Res("const")
        r_vec = Res("vec")
        r_ring = [Res(f"ring{i}") for i in range(NR)]
        r_bank = [Res(f"bank{i}") for i in range(8)]
        r_wbf = [Res(f"wbf{i}") for i in range(depth * NWL)]
        r_xt = [Res(f"xt{i}") for i in range(NT)]
        r_qk = [Res(f"qk{i}") for i in range(NT)]
        r_v = [Res(f"v{i}") for i in range(NT)]
        r_gc = [Res(f"gc{i}") for i in range(NT)]
        r_sga = [Res(f"sga{i}") for i in range(NT)]
        r_ot = [[Res(f"ot{h}_{i}") for i in range(NT)] for h in range(N_HEADS)]
        r_rope = Res("rope")

        ones_bf = cbf[:, 0:128]
        blk_bf = cbf[:, 128:256]
        rot_bf = cbf[:, 256:384]
        tri_bf = cbf[:, 384:512]
        tri2_bf = cbf[:, 384:640].rearrange("p (a n) -> p a n", a=2)

        wstate = {"n": 0}

        def wnext(chunk_idx):
            i = wstate["n"] % NR
            wstate["n"] += 1
            L = _chunk_len(chunk_idx)
            K.dma(SPQ, ring[i][:, 0:L], WBF.ap()[chunk_idx, :, 0:L],
                  reads=[r_wbf[chunk_idx]], writes=[r_ring[i]])
            return ring[i], r_ring[i]

        def mm(out, lhsT, rhs, start, stop, reads, writes, signal=False):
            return K.op(PE, lambda e: e.matmul(out, lhsT, rhs, start=start, stop=stop),
                        reads=reads, writes=writes, signal=signal)

        class Precast:
            def __init__(self, stack, ns, engines):
                self.ns = ns
                self.st32 = [sb(stack, f"st32_{i}", [128, WX], F32) for i in range(ns)]
                self.st16 = [sb(stack, f"st16_{i}", [128, WX], BF16) for i in range(ns)]
                self.r32 = [Res(f"st32_{i}") for i in range(ns)]
                self.r16 = [Res(f"st16_{i}") for i in range(ns)]
                self.engines = engines
                self.k = 0

            def chunk(self, n):
                i = self.k % self.ns
                eng = self.engines[self.k % len(self.engines)]
                self.k += 1
                L = _chunk_len(n)
                st32, st16 = self.st32[i], self.st16[i]
                K.dma(SPQ, st32[:, 0:L], wpk.ap()[n, :, 0:L], writes=[self.r32[i]])
                if eng is ACT:
                    K.op(ACT, lambda e: e.activation(out=st16[:, 0:L], in_=st32[:, 0:L], func=AF.Copy),
                         reads=[self.r32[i]], writes=[self.r16[i]])
                else:
                    K.op(eng, lambda e: e.tensor_copy(out=st16[:, 0:L], in_=st32[:, 0:L]),
                         reads=[self.r32[i]], writes=[self.r16[i]])
                K.dma(STQ, WBF.ap()[n, :, 0:L], st16[:, 0:L], reads=[self.r16[i]], writes=[r_wbf[n]])

        with ExitStack() as ps:
            pc0 = Precast(ps, 3, [ACT, DVE])
            pc0_next = {"n": 0}

            def pc0_step(k=1):
                for _ in range(k):
                    if pc0_next["n"] < NWL:
                        pc0.chunk(pc0_next["n"])
                        pc0_next["n"] += 1

            cst = sb(ps, "cst", [128, 640], F32)
            K.dma(SPQ, ident[:], ident_in.ap(), writes=[r_const])
            K.dma(SPQ, cst[:], cmat_in.ap(), writes=[r_const])
            K.dma(SPQ, invf[:], invf_in.ap(), writes=[r_const])
            K.op(ACT, lambda e: e.activation(out=cbf[:], in_=cst[:], func=AF.Copy),
                 reads=[r_const], writes=[r_const])

            vst = sb(ps, "vst", [128, 128], F32)
            r_vst = Res("vst")
            for l in range(depth):
                K.dma(SPQ, vst[:], vecs.ap()[l], writes=[r_vst])
                K.op(PE, lambda e: e.transpose(banks[7][:, 0:128], vst[:], ident[:]),
                     reads=[r_vst, r_const], writes=[r_bank[7]])
                K.op(DVE, lambda e, l=l: e.tensor_copy(out=VECT[:, l, :], in_=banks[7][:, 0:128]),
                     reads=[r_bank[7]], writes=[r_vec])
            cst8 = sb(ps, "cst8", [8, 128], F32)
            cact = sb(ps, "cact", [128, 8], F32)
            r_c = Res("c")
            K.dma(SPQ, cst8[:], c_in.ap(), writes=[r_c])
            K.op(PE, lambda e: e.transpose(banks[7][:, 0:8], cst8[:], ident[0:8, 0:8]),
                 reads=[r_c, r_const], writes=[r_bank[7]])
            K.op(ACT, lambda e: e.activation(out=cact[:], in_=banks[7][:, 0:8], func=AF.Silu),
                 reads=[r_bank[7]], writes=[r_c])
            wst = [sb(ps, f"wst{i}", [128, 4096], F32) for i in range(2)]
            r_wst = [Res(f"wst{i}") for i in range(2)]
            for l in range(depth):
                for jj in range(18):
                    bi = (l * 18 + jj) % 2
                    pc0_step(1)
                    K.dma(SPQ, wst[bi][:], wada.ap()[l * 18 + jj], writes=[r_wst[bi]])
                    wv = wst[bi][:].rearrange("p (c n) -> p c n", c=8)
                    for j4 in range(4):
                        col = jj * 4 + j4
                        for c in range(8):
                            mm(banks[6][:, col:col + 1], wv[:, c, j4 * 128:(j4 + 1) * 128], cact[:, c:c + 1],
                               start=(c == 0), stop=(c == 7), reads=[r_wst[bi], r_c], writes=[r_bank[6]],
                               signal=(c == 7 and j4 == 3))
                K.op(DVE, lambda e, l=l: e.tensor_tensor(out=MODB[:, l, :], in0=banks[6][:, 0:72],
                                                         in1=VECT[:, l, R_BADA:R_BADA + 72], op=ALU.add),
                     reads=[r_bank[6], r_vec], writes=[r_vec])
                for s in range(3):
                    K.op(DVE, lambda e, l=l, s=s: e.scalar_tensor_tensor(
                        out=AV[:, l, s * 8:(s + 1) * 8], in0=MODB[:, l, (3 * s + 1) * 8:(3 * s + 2) * 8], scalar=1.0,
                        in1=VECT[:, l, R_NG + s * 8:R_NG + (s + 1) * 8], op0=ALU.add, op1=ALU.mult),
                        reads=[r_vec], writes=[r_vec])
                    gsc = 1.0 if s == 1 else 0.5
                    K.op(DVE, lambda e, l=l, s=s, gsc=gsc: e.tensor_scalar(
                        out=GV[:, l, s * 8:(s + 1) * 8], in0=MODB[:, l, (3 * s + 2) * 8:(3 * s + 3) * 8],
                        scalar1=gsc, scalar2=None, op0=ALU.mult),
                        reads=[r_vec], writes=[r_vec])
                lam_init = 0.8 - 0.6 * float(np.exp(-0.3 * l))
                K.op(DVE, lambda e, l=l: e.tensor_scalar(out=QG[:, l:l + 1], in0=VECT[:, l, R_QG:R_QG + 1],
                                                         scalar1=HEAD_DIM ** -0.5, scalar2=None, op0=ALU.mult),
                     reads=[r_vec], writes=[r_vec])
                K.op(DVE, lambda e, l=l, li=lam_init: e.tensor_scalar(
                    out=SUBG[:, l:l + 1], in0=VECT[:, l, R_SUB:R_SUB + 1], scalar1=1.0 - li, scalar2=None,
                    op0=ALU.mult), reads=[r_vec], writes=[r_vec])
            lsb = sb(ps, "lsb", [128, depth * 256], F32)
            ljunk = sb(ps, "ljunk", [128, 64], F32)
            lacc = sb(ps, "lacc", [128, 2 * depth], F32)
            r_l = Res("lam")
            K.dma(SPQ, lsb[:], bass.AP(lamv, 0, [[0, 128], [1, depth * 256]]), writes=[r_l])
            for l in range(depth):
                for j in range(2):
                    o0 = l * 256 + j * 128
                    K.op(DVE, lambda e, o0=o0: e.tensor_tensor(
                        out=ljunk[:], in0=lsb[:, o0:o0 + 64], in1=lsb[:, o0 + 64:o0 + 128], op=ALU.mult),
                        reads=[r_l], writes=[r_l])
                    K.op(DVE, lambda e, l=l, j=j: e.tensor_reduce(
                        out=lacc[:, 2 * l + j:2 * l + j + 1], in_=ljunk[:], axis=mybir.AxisListType.X, op=ALU.add),
                        reads=[r_l], writes=[r_l])
            K.op(ACT, lambda e: e.activation(out=lacc[:], in_=lacc[:], func=AF.Exp), reads=[r_l], writes=[r_l])
            for l in range(depth):
                lam_init = 0.8 - 0.6 * float(np.exp(-0.3 * l))
                K.op(DVE, lambda e, l=l, li=lam_init: e.scalar_tensor_tensor(
                    out=NEGLAM[:, l:l + 1], in0=lacc[:, 2 * l + 1:2 * l + 2], scalar=-li,
                    in1=lacc[:, 2 * l:2 * l + 1], op0=ALU.add, op1=ALU.subtract),
                    reads=[r_l], writes=[r_vec])
            pc0_step(NWL)
            K.barrier()

        with ExitStack() as ps:
            RC = 1024 if S >= 1024 else S
            posi = sb(ps, "posi", [128, RC], I32)
            ang = sb(ps, "ang", [128, RC], F32)
            ta = sb(ps, "ta", [128, RC], F32)
            tb_ = sb(ps, "tb", [128, RC], F32)
            ti = sb(ps, "ti", [128, RC], I32)
            r_p = Res("rp")
            C1 = 6.28125
            C2 = TWO_PI - C1
            for ci in range(S // RC):
                K.dma(SPQ, posi[:], bass.AP(pos_in, ci * RC, [[0, 128], [1, RC]]), reads=[], writes=[r_p])
                K.op(DVE, lambda e: e.tensor_copy(out=ang[:], in_=posi[:]), reads=[r_p], writes=[r_p])
                K.op(DVE, lambda e: e.tensor_scalar(out=ang[:], in0=ang[:], scalar1=invf[:, 0:1], scalar2=None,
                                                    op0=ALU.mult), reads=[r_p, r_const], writes=[r_p])
                for which, shift, dst in (("sin", 0.0, SIN), ("cos", math.pi / 2, COS)):
                    K.op(DVE, lambda e, shift=shift: e.tensor_scalar(out=ta[:], in0=ang[:], scalar1=shift,
                                                                     scalar2=1.0 / TWO_PI, op0=ALU.add, op1=ALU.mult),
                         reads=[r_p], writes=[r_p])
                    K.op(DVE, lambda e: e.tensor_copy(out=ti[:], in_=ta[:]), reads=[r_p], writes=[r_p])
                    K.op(DVE, lambda e: e.tensor_copy(out=ta[:], in_=ti[:]), reads=[r_p], writes=[r_p])
                    K.op(DVE, lambda e, shift=shift: e.tensor_scalar(out=tb_[:], in0=ang[:], scalar1=shift,
                                                                     scalar2=None, op0=ALU.add),
                         reads=[r_p], writes=[r_p])
                    K.op(DVE, lambda e: e.scalar_tensor_tensor(out=tb_[:], in0=ta[:], scalar=-C1, in1=tb_[:],
                                                               op0=ALU.mult, op1=ALU.add), reads=[r_p], writes=[r_p])
                    K.op(DVE, lambda e: e.scalar_tensor_tensor(out=tb_[:], in0=ta[:], scalar=-C2, in1=tb_[:],
                                                               op0=ALU.mult, op1=ALU.add), reads=[r_p], writes=[r_p])
                    K.op(DVE, lambda e: e.tensor_scalar(out=ta[:], in0=tb_[:], scalar1=math.pi, scalar2=-TWO_PI,
                                                        op0=ALU.is_gt, op1=ALU.mult), reads=[r_p], writes=[r_p])
                    K.op(DVE, lambda e: e.tensor_tensor(out=tb_[:], in0=tb_[:], in1=ta[:], op=ALU.add),
                         reads=[r_p], writes=[r_p])
                    K.op(DVE, lambda e: e.tensor_scalar(out=ta[:], in0=tb_[:], scalar1=-math.pi, scalar2=TWO_PI,
                                                        op0=ALU.is_lt, op1=ALU.mult), reads=[r_p], writes=[r_p])
                    K.op(DVE, lambda e: e.tensor_tensor(out=tb_[:], in0=tb_[:], in1=ta[:], op=ALU.add),
                         reads=[r_p], writes=[r_p])
                    K.op(DVE, lambda e: e.tensor_scalar(out=tb_[:], in0=tb_[:], scalar1=3.14159, scalar2=-3.14159,
                                                        op0=ALU.min, op1=ALU.max), reads=[r_p], writes=[r_p])
                    K.op(ACT, lambda e: e.activation(out=tb_[:], in_=tb_[:], func=AF.Sin), reads=[r_p], writes=[r_p])
                    K.dma(STQ, dst.ap()[:, ci * RC:(ci + 1) * RC], tb_[:], reads=[r_p], writes=[r_rope])
            K.barrier()


        with ExitStack() as ps:
            xin = [sb(ps, f"xin{i}", [128, 4, D], F32) for i in range(2)]
            xo = [sb(ps, f"xo{i}", [128, 8, T], F32) for i in range(2)]
            r_xin = [Res(f"xin{i}") for i in range(2)]
            r_xo = [Res(f"xo{i}") for i in range(2)]
            xv = x_in.ap().rearrange("(n p) d -> p n d", p=128)
            for t in range(NT):
                b = t % 2
                K.dma(SPQ, xin[b][:], xv[:, t * 4:(t + 1) * 4, :], writes=[r_xin[b]])
                for c in range(8):
                    bk = c % 2
                    for tb in range(4):
                        K.op(PE, lambda e, b=b, c=c, tb=tb, bk=bk: e.transpose(
                            banks[bk][:, tb * 128:(tb + 1) * 128], xin[b][:, tb, c * 128:(c + 1) * 128], ident[:]),
                            reads=[r_xin[b], r_const], writes=[r_bank[bk]], signal=(tb == 3))
                    if c % 2 == 0:
                        K.op(ACT, lambda e, b=b, c=c, bk=bk: e.activation(out=xo[b][:, c, :], in_=banks[bk][:],
                                                                         func=AF.Copy),
                             reads=[r_bank[bk]], writes=[r_xo[b]])
                    else:
                        K.op(DVE, lambda e, b=b, c=c, bk=bk: e.tensor_copy(out=xo[b][:, c, :], in_=banks[bk][:]),
                             reads=[r_bank[bk]], writes=[r_xo[b]])
                K.dma(STQ, fm(XT)[:, :, t * T:(t + 1) * T], xo[b][:], reads=[r_xo[b]], writes=[r_xt[t]])
            K.barrier()

        for R in (r_const, r_vec, r_rope):
            R.frozen = True

        def rms_adaln(l, s, xb, r_xb, h, r_h, sq, r_sq, std, rstd, tmp, r_t):
            K.op(ACT, lambda e: e.activation(out=sq[:].rearrange("p c s -> p (c s)"),
                                             in_=xb[:].rearrange("p c s -> p (c s)"), func=AF.Square),
                 reads=[r_xb], writes=[r_sq])
            for c in range(8):
                mm(banks[0][:], ones_bf, sq[:, c, :], start=(c == 0), stop=(c == 7),
                   reads=[r_sq, r_const], writes=[r_bank[0]], signal=(c == 7))
            K.op(ACT, lambda e: e.activation(out=std[:], in_=banks[0][:], func=AF.Ln, bias=eps_t[:, 0:1],
                                             scale=1.0 / D),
                 reads=[r_bank[0]], writes=[r_t])
            K.op(ACT, lambda e: e.activation(out=rstd[:], in_=std[:], func=AF.Exp, scale=-0.5),
                 reads=[r_t], writes=[r_t])
            for c in range(8):
                K.op(DVE, lambda e, c=c: e.tensor_tensor(out=tmp[c % 2][:], in0=xb[:, c, :], in1=rstd[:], op=ALU.mult),
                     reads=[r_xb, r_t], writes=[r_tmp[c % 2]])
                K.op(ACT, lambda e, c=c: e.activation(
                    out=h[:, c, :], in_=tmp[c % 2][:], func=AF.Identity,
                    bias=MODB[:, l, 3 * s * 8 + c:3 * s * 8 + c + 1], scale=AV[:, l, s * 8 + c:s * 8 + c + 1]),
                    reads=[r_tmp[c % 2], r_vec], writes=[r_h])

        r_const.frozen = False
        eps_t = sb(es, "eps_t", [128, 1], F32)
        K.op(DVE, lambda e: e.memset(eps_t[:], EPS), writes=[r_const])
        ones_f32 = sb(es, "ones_f32", [128, 128], F32)
        K.op(DVE, lambda e: e.memset(ones_f32[:], 1.0), writes=[r_const])
        r_const.frozen = True
        r_tmp = [Res("tmp0"), Res("tmp1")]

        def ffn_phase(l, s, cbase):
            with ExitStack() as ps:
                xb_ = [sb(ps, f"xb{i}", [128, 8, T], F32) for i in range(2)]
                r_x = [Res(f"xb{i}") for i in range(2)]
                h = sb(ps, "h", [128, 8, T], BF16)
                sq = sb(ps, "sq", [128, 8, T], BF16)
                std = sb(ps, "std", [128, T], F32)
                rstd = sb(ps, "rstd", [128, T], F32)
                tmp = [sb(ps, f"tmp{i}", [128, T], F32) for i in range(2)]
                act = sb(ps, "act", [128, NF, T], BF16)
                sg = [sb(ps, f"sg{i}", [128, T], F32) for i in range(2)]
                r_h, r_sq, r_t, r_act = Res("h"), Res("sq"), Res("t"), Res("act")
                r_sg = [Res("sg0"), Res("sg1")]
                r_tmp[0], r_tmp[1] = Res("tmp0"), Res("tmp1")
                pcn = Precast(ps, 2, [K.POOL]) if l + 1 < depth else None
                half = NWL // 2
                pc_lo = 0 if s == 0 else half
                pcs = {"n": pc_lo, "end": pc_lo + half}

                def pc_step(k):
                    for _ in range(k):
                        if pcn is not None and pcs["n"] < pcs["end"]:
                            pcn.chunk((l + 1) * NWL + pcs["n"])
                            pcs["n"] += 1

                def load(t):
                    K.dma(SPQ, xb_[t % 2][:], fm(XT)[:, :, t * T:(t + 1) * T], reads=[r_xt[t]], writes=[r_x[t % 2]])

                load(0)
                rms_adaln(l, s, xb_[0], r_x[0], h, r_h, sq, r_sq, std, rstd, tmp, r_t)
                for t in range(NT):
                    xb, r_xb = xb_[t % 2], r_x[t % 2]
                    for j in range(11):
                        if j == 3 and t + 1 < NT:
                            load(t + 1)
                        if (t * 11 + j) % 2 == 0:
                            pc_step(1)
                        slot, r_s = wnext(cbase + j)
                        sv = slot[:].rearrange("p (c h n) -> p c h n", c=8, h=2)
                        for f2 in range(2):
                            f = 2 * j + f2
                            gb, ub = 1 + f % 2, 3 + f % 2
                            for c in range(8):
                                mm(banks[gb][:], sv[:, c, 0, f2 * 128:(f2 + 1) * 128], h[:, c, :],
                                   start=(c == 0), stop=(c == 7), reads=[r_s, r_h], writes=[r_bank[gb]],
                                   signal=(c == 7))
                            for c in range(8):
                                mm(banks[ub][:], sv[:, c, 1, f2 * 128:(f2 + 1) * 128], h[:, c, :],
                                   start=(c == 0), stop=(c == 7), reads=[r_s, r_h], writes=[r_bank[ub]],
                                   signal=(c == 7))
                            K.op(ACT, lambda e, f=f, gb=gb: e.activation(out=sg[f % 2][:], in_=banks[gb][:], func=AF.Silu),
                                 reads=[r_bank[gb]], writes=[r_sg[f % 2]])
                            K.op(DVE, lambda e, f=f, ub=ub: e.tensor_tensor(out=act[:, f, :], in0=banks[ub][:],
                                                                           in1=sg[f % 2][:], op=ALU.mult),
                                 reads=[r_bank[ub], r_sg[f % 2]], writes=[r_act])
                    for dc in range(8):
                        slot, r_s = wnext(cbase + 11 + dc)
                        sv = slot[:, 0:NF * 128].rearrange("p (f n) -> p f n", f=NF)
                        ob = 5 + dc % 2
                        for f in range(NF):
                            mm(banks[ob][:], sv[:, f, :], act[:, f, :], start=(f == 0), stop=(f == NF - 1),
                               reads=[r_s, r_act], writes=[r_bank[ob]], signal=(f == NF - 1))
                        K.op(DVE, lambda e, dc=dc, ob=ob: e.scalar_tensor_tensor(
                            out=xb[:, dc, :], in0=banks[ob][:], scalar=GV[:, l, s * 8 + dc:s * 8 + dc + 1],
                            in1=xb[:, dc, :], op0=ALU.mult, op1=ALU.add),
                            reads=[r_bank[ob], r_xb, r_vec], writes=[r_xb])
                        if dc == 3 and t + 1 < NT:
                            rms_adaln(l, s, xb_[(t + 1) % 2], r_x[(t + 1) % 2], h, r_h, sq, r_sq, std, rstd, tmp, r_t)
                    K.dma(STQ, fm(XT)[:, :, t * T:(t + 1) * T], xb[:], reads=[r_xb], writes=[r_xt[t]])
                pc_step(NWL)
                K.barrier()

        def m1_phase(l, cbase):
            with ExitStack() as ps:
                xb1 = sb(ps, "xb0", [128, 8, T], F32)
                rx1 = Res("xb0")
                xb_ = [xb1, xb1]
                r_x = [rx1, rx1]
                h = sb(ps, "h", [128, 8, T], BF16)
                sq = sb(ps, "sq", [128, 8, T], BF16)
                std = sb(ps, "std", [128, T], F32)
                rstd = sb(ps, "rstd", [128, T], F32)
                tmp = [sb(ps, f"tmp{i}", [128, T], F32) for i in range(2)]
                r_h, r_sq, r_t = Res("h"), Res("sq"), Res("t")
                r_tmp[0], r_tmp[1] = Res("tmp0"), Res("tmp1")
                U = sb(ps, "U", [128, 8, T + 2], F32)
                r_U = Res("U")
                yci = sb(ps, "yci", [128, 8, T], BF16)
                sgc = sb(ps, "sgc", [128, 8, T], BF16)
                r_yci, r_sgc = Res("yci"), Res("sgc")
                NO = 4
                obuf = [sb(ps, f"obuf{i}", [128, 4, T], BF16) for i in range(NO)]
                r_ob = [Res(f"obuf{i}") for i in range(NO)]
                ost = {"n": 0}
                cs = [sb(ps, f"cs{i}", [128, 2, T], F32) for i in range(2)]
                r_cs = [Res(f"cs{i}") for i in range(2)]
                sqb = [sb(ps, f"sqb{i}", [128, T], BF16) for i in range(3)]
                qln = [sb(ps, f"qln{i}", [128, T], F32) for i in range(2)]
                qrs = [sb(ps, f"qrs{i}", [128, T], F32) for i in range(2)]
                qnb = [sb(ps, f"qnb{i}", [128, T], BF16) for i in range(3)]
                r_sqb = [Res(f"sqb{i}") for i in range(3)]
                r_qln = [Res(f"qln{i}") for i in range(2)]
                r_qrs = [Res(f"qrs{i}") for i in range(2)]
                r_qnb = [Res(f"qnb{i}") for i in range(3)]
                t1 = sb(ps, "t1", [128, T], F32)
                t2 = sb(ps, "t2", [128, T], F32)
                xcs = sb(ps, "xcs", [128, T], F32)
                acc = sb(ps, "acc", [128, T], F32)
                r_q = {k: Res(k) for k in ("qsb", "sqb", "qstd", "qrs", "qn", "qnb", "t1", "t2", "xcs", "acc")}

                K.op(DVE, lambda e: e.memset(U[:, :, 0:2], 0.0), writes=[r_U])

                def load(t):
                    b = t % 2
                    K.dma(SPQ, xb_[b][:], fm(XT)[:, :, t * T:(t + 1) * T], reads=[r_xt[t]], writes=[r_x[b]])
                    K.dma(SPQ, cs[b][:, 0, :], COS.ap()[:, t * T:(t + 1) * T], reads=[r_rope], writes=[r_cs[b]])
                    K.dma(SPQ, cs[b][:, 1, :], SIN.ap()[:, t * T:(t + 1) * T], reads=[r_rope], writes=[r_cs[b]])

                def next_ob():
                    i = ost["n"] % NO
                    ost["n"] += 1
                    return obuf[i], r_ob[i]

                load(0)
                pp = {"n": 0}

                def pbank():
                    b = 1 + pp["n"] % 2
                    pp["n"] += 1
                    return b

                rms_adaln(l, 1, xb_[0], r_x[0], h, r_h, sq, r_sq, std, rstd, tmp, r_t)
                for t in range(NT):
                    xb, r_xb = xb_[t % 2], r_x[t % 2]
                    csb, r_csb = cs[t % 2], r_cs[t % 2]
                    tsl = slice(t * T, (t + 1) * T)
                    heads = [(qk, j2, j) for qk in range(2) for j2 in range(2) for j in range(4)]
                    PB = (1, 2, 5, 6)
                    hctx = {}

                    def stA(n):
                        qk, j2, j = heads[n]
                        if j == 0:
                            slot, r_s = wnext(cbase + qk * 2 + j2)
                            ob, r_o = next_ob()
                            hctx[(qk, j2)] = (slot, r_s, ob, r_o)
                        slot, r_s, ob, r_o = hctx[(qk, j2)]
                        sv = slot[:].rearrange("p (c n) -> p c n", c=8)
                        pb = PB[n % 4]
                        for c in range(8):
                            mm(banks[pb][:], sv[:, c, j * 128:(j + 1) * 128], h[:, c, :],
                               start=(c == 0), stop=(c == 7), reads=[r_s, r_h], writes=[r_bank[pb]],
                               signal=(c == 7))
                        K.op(ACT, lambda e: e.activation(out=sqb[n % 3][:], in_=banks[pb][:], func=AF.Square),
                             reads=[r_bank[pb]], writes=[r_sqb[n % 3]])

                    def stC(n):
                        qk, j2, j = heads[n]
                        pb = PB[n % 4]
                        i = n % 2
                        i3 = n % 3
                        mm(banks[3][:], blk_bf, sqb[i3][:], start=True, stop=True,
                           reads=[r_sqb[i3], r_const], writes=[r_bank[3]], signal=True)
                        K.op(ACT, lambda e: e.activation(out=qln[i][:], in_=banks[3][:], func=AF.Ln,
                                                         bias=eps_t[:, 0:1], scale=1.0 / HEAD_DIM),
                             reads=[r_bank[3]], writes=[r_qln[i]])
                        K.op(ACT, lambda e: e.activation(out=qrs[i][:], in_=qln[i][:], func=AF.Exp, scale=-0.5),
                             reads=[r_qln[i]], writes=[r_qrs[i]])
                        gain = QG[:, l:l + 1] if qk == 0 else VECT[:, l, R_KG:R_KG + 1]
                        K.op(DVE, lambda e: e.scalar_tensor_tensor(
                            out=qnb[i3][:], in0=banks[pb][:], scalar=gain, in1=qrs[i][:], op0=ALU.mult, op1=ALU.mult),
                            reads=[r_bank[pb], r_qrs[i], r_vec], writes=[r_qnb[i3]])

                    def stF(n):
                        qk, j2, j = heads[n]
                        i = n % 3
                        slot, r_s, ob, r_o = hctx[(qk, j2)]
                        mm(banks[4][:], rot_bf, qnb[i][:], start=True, stop=True,
                           reads=[r_qnb[i], r_const], writes=[r_bank[4]], signal=True)
                        K.op(DVE, lambda e: e.tensor_tensor(out=t1[:], in0=qnb[i][:], in1=csb[:, 0, :], op=ALU.mult),
                             reads=[r_qnb[i], r_csb], writes=[r_q["t1"]])
                        K.op(DVE, lambda e: e.tensor_tensor(out=t2[:], in0=banks[4][:], in1=csb[:, 1, :], op=ALU.mult),
                             reads=[r_bank[4], r_csb], writes=[r_q["t2"]])
                        K.op(DVE, lambda e: e.tensor_tensor(out=ob[:, j, :], in0=t1[:], in1=t2[:], op=ALU.add),
                             reads=[r_q["t1"], r_q["t2"]], writes=[r_o])
                        if j == 3:
                            dst = QT if qk == 0 else KT
                            K.dma(STQ, fm(dst)[:, j2 * 4:(j2 + 1) * 4, tsl], ob[:], reads=[r_o], writes=[r_qk[t]])

                    vv = VV.ap().rearrange("(n p) d -> p n d", p=128)
                    vctx = {}

                    def v_item(m):
                        j2, tb = divmod(m, 4)
                        if tb == 0:
                            slot, r_s = wnext(cbase + 4 + j2)
                            ob, r_o = next_ob()
                            vctx[j2] = (slot, r_s, ob, r_o)
                        slot, r_s, ob, r_o = vctx[j2]
                        sv = slot[:].rearrange("p (c n) -> p c n", c=8)
                        pb = (7, 0)[m % 2]
                        for c in range(8):
                            mm(banks[pb][:], h[:, c, tb * 128:(tb + 1) * 128], sv[:, c, :],
                               start=(c == 0), stop=(c == 7), reads=[r_s, r_h], writes=[r_bank[pb]],
                               signal=(c == 7))
                        K.op(ACT, lambda e: e.activation(out=ob[:, tb, :], in_=banks[pb][:], func=AF.Copy),
                             reads=[r_bank[pb]], writes=[r_o])
                        if tb == 3:
                            K.dma(STQ, vv[:, t * 4:(t + 1) * 4, j2 * 512:(j2 + 1) * 512], ob[:], reads=[r_o],
                                  writes=[r_v[t]])

                    for n in range(16 + 8):
                        if n < 16:
                            stA(n)
                        else:
                            v_item(n - 16)
                        if 0 <= n - 2 < 16:
                            stC(n - 2)
                        if 0 <= n - 4 < 16:
                            stF(n - 4)
                    if t + 1 < NT:
                        load(t + 1)
                    for j2 in range(2):
                        slot, r_s = wnext(cbase + 6 + j2)
                        sv = slot[:].rearrange("p (c n) -> p c n", c=8)
                        ob, r_o = next_ob()
                        for j in range(4):
                            pb = pbank()
                            for c in range(8):
                                mm(banks[pb][:], sv[:, c, j * 128:(j + 1) * 128], h[:, c, :],
                                   start=(c == 0), stop=(c == 7), reads=[r_s, r_h], writes=[r_bank[pb]],
                                   signal=(c == 7))
                            K.op(ACT, lambda e, pb=pb, j=j, ob=ob: e.activation(out=ob[:, j, :], in_=banks[pb][:],
                                                                               func=AF.Sigmoid),
                                 reads=[r_bank[pb]], writes=[r_o])
                        K.dma(STQ, fm(SGA)[:, j2 * 4:(j2 + 1) * 4, tsl], ob[:], reads=[r_o], writes=[r_sga[t]])
                    for i in range(8):
                        slot, r_s = wnext(cbase + 8 + i)
                        sv = slot[:].rearrange("p (c n) -> p c n", c=8)
                        bset = (1, 2, 3, 4) if i % 2 == 0 else (5, 6, 7, 0)
                        for j in range(4):
                            pb = bset[j]
                            for c in range(8):
                                mm(banks[pb][:], sv[:, c, j * 128:(j + 1) * 128], h[:, c, :],
                                   start=(c == 0), stop=(c == 7), reads=[r_s, r_h], writes=[r_bank[pb]],
                                   signal=(c == 7))
                        bB, bC, bX, bG = bset
                        K.op(ACT, lambda e, bX=bX: e.activation(out=xcs[:], in_=banks[bX][:], func=AF.Copy),
                             reads=[r_bank[bX]], writes=[r_q["xcs"]])
                        K.op(DVE, lambda e, i=i, bC=bC: e.tensor_tensor(out=U[:, i, 2:T + 2], in0=banks[bC][:], in1=xcs[:],
                                                                       op=ALU.mult),
                             reads=[r_bank[bC], r_q["xcs"]], writes=[r_U])
                        cw = lambda k, i=i: VECT[:, l, R_CW + k * 8 + i:R_CW + k * 8 + i + 1]
                        K.op(DVE, lambda e, i=i: e.tensor_scalar(out=acc[:], in0=U[:, i, 2:T + 2], scalar1=cw(2),
                                                                 scalar2=None, op0=ALU.mult),
                             reads=[r_U, r_vec], writes=[r_q["acc"]])
                        K.op(DVE, lambda e, i=i: e.scalar_tensor_tensor(out=acc[:], in0=U[:, i, 1:T + 1], scalar=cw(1),
                                                                        in1=acc[:], op0=ALU.mult, op1=ALU.add),
                             reads=[r_U, r_vec, r_q["acc"]], writes=[r_q["acc"]])
                        K.op(DVE, lambda e, i=i: e.scalar_tensor_tensor(out=acc[:], in0=U[:, i, 0:T], scalar=cw(0),
                                                                        in1=acc[:], op0=ALU.mult, op1=ALU.add),
                             reads=[r_U, r_vec, r_q["acc"]], writes=[r_q["acc"]])
                        K.op(DVE, lambda e, i=i, bB=bB: e.tensor_tensor(out=yci[:, i, :], in0=banks[bB][:], in1=acc[:],
                                                                       op=ALU.mult),
                             reads=[r_bank[bB], r_q["acc"]], writes=[r_yci])
                        K.op(ACT, lambda e, i=i, bG=bG: e.activation(out=sgc[:, i, :], in_=banks[bG][:], func=AF.Sigmoid),
                             reads=[r_bank[bG]], writes=[r_sgc])
                    K.op(DVE, lambda e: e.tensor_copy(out=U[:, :, 0:2], in_=U[:, :, T:T + 2]), reads=[r_U], writes=[r_U])
                    if t + 1 < NT:
                        rms_adaln(l, 1, xb_[(t + 1) % 2], r_x[(t + 1) % 2], h, r_h, sq, r_sq, std, rstd, tmp, r_t)
                    for j2 in range(2):
                        slot, r_s = wnext(cbase + 16 + j2)
                        sv = slot[:].rearrange("p (c n) -> p c n", c=8)
                        ob, r_o = next_ob()
                        for j in range(4):
                            pb = pbank()
                            dc = j2 * 4 + j
                            for c in range(8):
                                mm(banks[pb][:], sv[:, c, j * 128:(j + 1) * 128], yci[:, c, :],
                                   start=(c == 0), stop=(c == 7), reads=[r_s, r_yci], writes=[r_bank[pb]],
                                   signal=(c == 7))
                            K.op(DVE, lambda e, pb=pb, j=j, dc=dc, ob=ob: e.tensor_tensor(
                                out=ob[:, j, :], in0=banks[pb][:], in1=sgc[:, dc, :], op=ALU.mult),
                                reads=[r_bank[pb], r_sgc], writes=[r_o])
                        K.dma(STQ, fm(GC)[:, j2 * 4:(j2 + 1) * 4, tsl], ob[:], reads=[r_o], writes=[r_gc[t]])
                K.barrier()

        def m2_phase(l):
            NQ = S // T
            with ExitStack() as ps:
                kt = [sb(ps, f"kt{i}", [128, S], BF16) for i in range(2)]
                qt = [sb(ps, f"qt{i}", [128, S], BF16) for i in range(2)]
                vh = [sb(ps, f"vh{i}", [128, NB, 128], BF16) for i in range(2)]
                r_kt = [Res(f"kt{i}") for i in range(2)]
                r_qt = [Res(f"qt{i}") for i in range(2)]
                r_vh = [Res(f"vh{i}") for i in range(2)]
                NP = 3
                pt = [sb(ps, f"pt{i}", [128, 2, T], BF16) for i in range(NP)]
                r_pt = [Res(f"pt{i}") for i in range(NP)]
                rl = [sb(ps, f"rl{i}", [128, 2 * T], F32) for i in range(2)]
                lnl = [sb(ps, f"lnl{i}", [128, 2 * T], F32) for i in range(2)]
                oc = [sb(ps, f"oc{i}", [128, 2 * T], F32) for i in range(2)]
                o = [sb(ps, f"o{i}", [128, T], F32) for i in range(2)]
                osq = [sb(ps, f"osq{i}", [128, T], BF16) for i in range(2)]
                ostd = [sb(ps, f"ostd{i}", [128, T], F32) for i in range(2)]
                on = [sb(ps, f"on{i}", [128, T], BF16) for i in range(2)]
                lacc = [sb(ps, f"lacc{i}", [128, T], F32) for i in range(2)]
                r_lacc = [Res(f"lacc{i}") for i in range(2)]
                r_m = {f"{nm}{i}": Res(f"{nm}{i}") for nm in ("rl", "ln", "oc", "o", "osq", "ostd", "on") for i in range(2)}
                pairs = ((0, 1), (6, 7))
                OB = (2, 3)
                LB = (4, 5)

                def pair_ap(pr):
                    return psall[:, pr[0] * 512:(pr[0] + 2) * 512].rearrange("p (a n) -> p a n", a=2)

                def load(hd):
                    b = hd % 2
                    K.dma(SPQ, kt[b][:], KT.ap()[hd * 128:(hd + 1) * 128, :], reads=r_qk, writes=[r_kt[b]])
                    K.dma(SPQ, qt[b][:], QT.ap()[hd * 128:(hd + 1) * 128, :], reads=r_qk, writes=[r_qt[b]])
                    vv = VV.ap().rearrange("(n p) d -> p n d", p=128)
                    K.dma(SPQ, vh[b][:], vv[:, :, hd * 128:(hd + 1) * 128], reads=r_v, writes=[r_vh[b]])

                its = [(hd, qb, kb) for hd in range(N_HEADS) for qb in range(NQ) for kb in range(4 * qb + 4)]
                cnt = {"s": 0, "p": 0}

                def emit_qk(it):
                    hd, qb, kb = it
                    b = hd % 2
                    r = kb - 4 * qb
                    c0 = r * 128 if r > 0 else 0
                    pr = pairs[cnt["s"] % 2]
                    cnt["s"] += 1
                    for comp in range(2):
                        ps_ = slice(comp * 64, (comp + 1) * 64)
                        mm(banks[pr[comp]][:, c0:T], kt[b][ps_, kb * 128:(kb + 1) * 128],
                           qt[b][ps_, qb * T + c0:(qb + 1) * T], start=True, stop=True,
                           reads=[r_kt[b], r_qt[b]], writes=[r_bank[pr[comp]]], signal=(comp == 1))
                    pi = cnt["p"] % NP
                    cnt["p"] += 1
                    K.op(ACT, lambda e: e.activation(out=pt[pi][:, :, c0:T], in_=pair_ap(pr)[:, :, c0:T], func=AF.Exp),
                         reads=[r_bank[pr[0]], r_bank[pr[1]]], writes=[r_pt[pi]])
                    if r >= 0:
                        K.op(DVE, lambda e: e.tensor_tensor(out=pt[pi][:, :, c0:c0 + 128], in0=pt[pi][:, :, c0:c0 + 128],
                                                            in1=tri2_bf, op=ALU.mult),
                             reads=[r_pt[pi], r_const], writes=[r_pt[pi]])
                    k = (hd * NQ + qb) % 2
                    if kb == 0:
                        K.op(DVE, lambda e: e.tensor_copy(out=lacc[k][:], in_=pt[pi][:, 0, :]),
                             reads=[r_pt[pi]], writes=[r_lacc[k]])
                    else:
                        K.op(DVE, lambda e: e.tensor_tensor(out=lacc[k][:, c0:T], in0=lacc[k][:, c0:T],
                                                            in1=pt[pi][:, 0, c0:T], op=ALU.add),
                             reads=[r_pt[pi], r_lacc[k]], writes=[r_lacc[k]])
                    return pi, c0

                deferred = []
                itn = {"n": 0, "bnd": 0}

                def emit_pv(it, pi, c0):
                    hd, qb, kb = it
                    b = hd % 2
                    nkb = 4 * qb + 4
                    last = kb == nkb - 1
                    if qb == 0 and kb == 0 and hd + 1 < N_HEADS:
                        load(hd + 1)
                    for comp in range(2):
                        mm(banks[OB[comp]][:, c0:T], vh[b][:, kb, :], pt[pi][:, comp, c0:T], start=(kb == 0), stop=last,
                           reads=[r_vh[b], r_pt[pi]], writes=[r_bank[OB[comp]]], signal=False)
                    mm(banks[LB[1]][:, c0:T], ones_bf, pt[pi][:, 1, c0:T], start=(kb == 0), stop=last,
                       reads=[r_const, r_pt[pi]], writes=[r_bank[LB[1]]], signal=True)
                    if last:
                        k = (hd * NQ + qb) % 2
                        mm(banks[LB[0]][:], ones_f32[:], lacc[k][:], start=True, stop=True,
                           reads=[r_const, r_lacc[k]], writes=[r_bank[LB[0]]], signal=True)
                        post(hd, qb)

                O2 = psall[:, OB[0] * 512:(OB[0] + 2) * 512]
                L2 = psall[:, LB[0] * 512:(LB[0] + 2) * 512]

                def post(hd, qb):
                    n0 = itn["n"]
                    k = (hd * NQ + qb) % 2
                    rk, lk, ok_, osqk, ostdk, onk = (r_m[f"rl{k}"], r_m[f"ln{k}"], r_m[f"o{k}"], r_m[f"osq{k}"],
                                                     r_m[f"ostd{k}"], r_m[f"on{k}"])
                    ock = r_m[f"oc{k}"]
                    K.op(DVE, lambda e: e.tensor_copy(out=oc[k][:], in_=O2), reads=[r_bank[OB[0]], r_bank[OB[1]]],
                         writes=[ock])
                    K.op(ACT, lambda e: e.activation(out=lnl[k][:], in_=L2, func=AF.Ln),
                         reads=[r_bank[LB[0]], r_bank[LB[1]]], writes=[lk])

                    def d1():
                        K.op(ACT, lambda e: e.activation(out=rl[k][:], in_=lnl[k][:], func=AF.Exp, scale=-1.0),
                             reads=[lk], writes=[rk])

                    def d2():
                        K.op(DVE, lambda e: e.tensor_tensor(out=oc[k][:], in0=oc[k][:], in1=rl[k][:], op=ALU.mult),
                             reads=[rk, ock], writes=[ock])
                        K.op(DVE, lambda e: e.scalar_tensor_tensor(out=o[k][:], in0=oc[k][:, T:2 * T],
                                                                   scalar=NEGLAM[:, l:l + 1], in1=oc[k][:, 0:T],
                                                                   op0=ALU.mult, op1=ALU.add),
                             reads=[ock, r_vec], writes=[ok_])

                    def d3():
                        K.op(ACT, lambda e: e.activation(out=osq[k][:], in_=o[k][:], func=AF.Square),
                             reads=[ok_], writes=[osqk])

                    def d4():
                        sbk = pairs[itn["n"] % 2][0]
                        mm(banks[sbk][:], ones_bf, osq[k][:], start=True, stop=True, reads=[osqk, r_const],
                           writes=[r_bank[sbk]], signal=True)
                        K.op(ACT, lambda e: e.activation(out=ostd[k][:], in_=banks[sbk][:], func=AF.Ln,
                                                         bias=eps_t[:, 0:1], scale=1.0 / 128.0),
                             reads=[r_bank[sbk]], writes=[ostdk])

                    def d5():
                        K.op(ACT, lambda e: e.activation(out=ostd[k][:], in_=ostd[k][:], func=AF.Exp, scale=-0.5),
                             reads=[ostdk], writes=[ostdk])
                        K.op(DVE, lambda e: e.scalar_tensor_tensor(
                            out=on[k][:], in0=o[k][:], scalar=SUBG[:, l:l + 1], in1=ostd[k][:], op0=ALU.mult,
                            op1=ALU.mult), reads=[ok_, ostdk, r_vec], writes=[onk])
                        K.dma(STQ, OT.ap()[hd * 128:(hd + 1) * 128, qb * T:(qb + 1) * T], on[k][:],
                              reads=[onk], writes=[r_ot[hd][qb]])

                    for dly, fn in ((1, d1), (2, d2), (3, d3), (4, d4), (5, d5)):
                        deferred.append((n0 + dly, fn))

                def run_deferred(upto):
                    while deferred and deferred[0][0] <= upto:
                        deferred.pop(0)[1]()

                load(0)
                qd = [emit_qk(its[i]) for i in range(min(2, len(its)))]
                for n in range(len(its)):
                    itn["n"] = n
                    run_deferred(n)
                    if n + 2 < len(its):
                        qd.append(emit_qk(its[n + 2]))
                    emit_pv(its[n], *qd[n])
                itn["n"] = len(its)
                run_deferred(10 ** 9)
                K.barrier()

        def m3_phase(l, cbase):
            with ExitStack() as ps:
                xb_ = [sb(ps, f"xb{i}", [128, 8, T], F32) for i in range(2)]
                otb = [sb(ps, f"otb{i}", [128, 8, T], BF16) for i in range(2)]
                gcb = [sb(ps, f"gcb{i}", [128, 8, T], BF16) for i in range(2)]
                sgb = [sb(ps, f"sgb{i}", [128, 8, T], BF16) for i in range(2)]
                r_x = [Res(f"xb{i}") for i in range(2)]
                r_o = [Res(f"otb{i}") for i in range(2)]
                r_g = [Res(f"gcb{i}") for i in range(2)]
                r_s_ = [Res(f"sgb{i}") for i in range(2)]
                mg = [sb(ps, f"mg{i}", [128, 8, T], BF16) for i in range(2)]
                r_mg = [Res("mg0"), Res("mg1")]
                tm = [sb(ps, f"tm{i}", [128, T], F32) for i in range(2)]
                r_tm = [Res("tm0"), Res("tm1")]

                def load_a(t):
                    b = t % 2
                    tsl = slice(t * T, (t + 1) * T)
                    K.dma(SPQ, otb[b][:], fm(OT)[:, :, tsl], reads=[r_ot[hd][t] for hd in range(N_HEADS)], writes=[r_o[b]])
                    K.dma(SPQ, sgb[b][:], fm(SGA)[:, :, tsl], reads=[r_sga[t]], writes=[r_s_[b]])
                    K.dma(SPQ, gcb[b][:], fm(GC)[:, :, tsl], reads=[r_gc[t]], writes=[r_g[b]])

                def load_x(t):
                    b = t % 2
                    K.dma(SPQ, xb_[b][:], fm(XT)[:, :, t * T:(t + 1) * T], reads=[r_xt[t]], writes=[r_x[b]])

                pp = {"n": 0}

                def stage_a(t):
                    b = t % 2
                    for j2 in range(2):
                        slot, r_s = wnext(cbase + j2)
                        sv = slot[:].rearrange("p (c n) -> p c n", c=8)
                        for j in range(4):
                            dc = j2 * 4 + j
                            pb = 1 + pp["n"] % 2
                            pp["n"] += 1
                            for c in range(8):
                                mm(banks[pb][:], sv[:, c, j * 128:(j + 1) * 128], otb[b][:, c, :],
                                   start=(c == 0), stop=(c == 7), reads=[r_s, r_o[b]], writes=[r_bank[pb]],
                                   signal=(c == 7))
                            K.op(DVE, lambda e: e.tensor_tensor(out=tm[dc % 2][:], in0=banks[pb][:],
                                                                in1=sgb[b][:, dc, :], op=ALU.mult),
                                 reads=[r_bank[pb], r_s_[b]], writes=[r_tm[dc % 2]])
                            K.op(DVE, lambda e: e.tensor_tensor(out=mg[b][:, dc, :], in0=tm[dc % 2][:],
                                                                in1=gcb[b][:, dc, :], op=ALU.add),
                                 reads=[r_tm[dc % 2], r_g[b]], writes=[r_mg[b]])

                def stage_w(t):
                    b = t % 2
                    for j2 in range(2):
                        slot, r_s = wnext(cbase + 2 + j2)
                        sv = slot[:].rearrange("p (c n) -> p c n", c=8)
                        for j in range(4):
                            dc = j2 * 4 + j
                            pb = 3 + pp["n"] % 2
                            pp["n"] += 1
                            for c in range(8):
                                mm(banks[pb][:], sv[:, c, j * 128:(j + 1) * 128], mg[b][:, c, :],
                                   start=(c == 0), stop=(c == 7), reads=[r_s, r_mg[b]], writes=[r_bank[pb]],
                                   signal=(c == 7))
                            K.op(DVE, lambda e: e.scalar_tensor_tensor(
                                out=xb_[b][:, dc, :], in0=banks[pb][:], scalar=GV[:, l, 8 + dc:8 + dc + 1],
                                in1=xb_[b][:, dc, :], op0=ALU.mult, op1=ALU.add),
                                reads=[r_bank[pb], r_x[b], r_vec], writes=[r_x[b]])
                    K.dma(STQ, fm(XT)[:, :, t * T:(t + 1) * T], xb_[b][:], reads=[r_x[b]], writes=[r_xt[t]])

                load_a(0)
                load_x(0)
                stage_a(0)
                for t in range(NT):
                    if t + 1 < NT:
                        load_a(t + 1)
                        stage_a(t + 1)
                    stage_w(t)
                    if t + 1 < NT:
                        load_x(t + 1)
                K.barrier()

        for l in range(depth):
            base = l * NWL
            if "f1" in stages:
                ffn_phase(l, 0, base + 0)
            if "m1" in stages:
                m1_phase(l, base + 19)
            if "m2" in stages:
                m2_phase(l)
            if "m3" in stages:
                m3_phase(l, base + 37)
            if "f2" in stages:
                ffn_phase(l, 2, base + 41)

        with ExitStack() as ps:
            xb_ = [sb(ps, f"xb{i}", [128, 8, T], F32) for i in range(2)]
            yo = [sb(ps, f"yo{i}", [128, D], F32) for i in range(2)]
            r_x = [Res(f"xb{i}") for i in range(2)]
            r_y = [Res(f"yo{i}") for i in range(2)]
            yv = y_out.ap().rearrange("(n p) d -> p n d", p=128)
            n = 0
            for t in range(NT):
                b = t % 2
                K.dma(SPQ, xb_[b][:], fm(XT)[:, :, t * T:(t + 1) * T], reads=[r_xt[t]], writes=[r_x[b]])
                for tb in range(4):
                    yb = n % 2
                    n += 1
                    for half in range(2):
                        bk = half
                        for c4 in range(4):
                            c = half * 4 + c4
                            K.op(PE, lambda e, b=b, c=c, tb=tb, bk=bk, c4=c4: e.transpose(
                                banks[bk][:, c4 * 128:(c4 + 1) * 128], xb_[b][:, c, tb * 128:(tb + 1) * 128], ident[:]),
                                reads=[r_x[b], r_const], writes=[r_bank[bk]], signal=(c4 == 3))
                        if half == 0:
                            K.op(ACT, lambda e, yb=yb, bk=bk: e.activation(out=yo[yb][:, 0:512], in_=banks[bk][:],
                                                                          func=AF.Copy),
                                 reads=[r_bank[bk]], writes=[r_y[yb]])
                        else:
                            K.op(DVE, lambda e, yb=yb, bk=bk: e.tensor_copy(out=yo[yb][:, 512:1024], in_=banks[bk][:]),
                                 reads=[r_bank[bk]], writes=[r_y[yb]])
                    K.dma(STQ, yv[:, t * 4 + tb, :], yo[yb][:], reads=[r_y[yb]], writes=[])
            K.barrier()
        print(f"[build] instructions emitted: {K.n_ins}")
    return nc


_CACHE = {}


def run(inputs, S, depth, stages=("f1", "m1", "m2", "m3", "f2"), n_cores=8, trace=False):
    key = (S, depth, tuple(stages))
    if key not in _CACHE:
        _CACHE[key] = build(S, depth, stages)
    nc = _CACHE[key]
    inp = {k: np.asarray(v) for k, v in inputs.items()}
    shared = _host_inputs(inp, depth)
    in_maps = []
    for b in range(n_cores):
        m = dict(shared)
        m["x"] = np.ascontiguousarray(inp["x"][b, :S], dtype=np.float32)
        m["c"] = np.ascontiguousarray(inp["c"][b].reshape(8, 128), dtype=np.float32)
        m["pos"] = np.ascontiguousarray(inp["positions"][b, :S].reshape(1, S), dtype=np.int32)
        in_maps.append(m)
    res = run_bass_kernel_spmd(nc, in_maps, core_ids=list(range(n_cores)), trace=trace)
    out = np.stack([np.asarray(r["y"]) for r in res.results], axis=0).astype(np.float32)
    return out, res


def kernel(**inputs):
    out, _ = run(inputs, 4096, 4)
    return out
```
